# Optimizing a Trainium2 kernel written in Bass

```python
import math
import jax, jax.numpy as jnp
from jax import lax
import numpy as np

D_MODEL = 1024
BATCH = 4
SEQ = 8192
DEPTH = 2

HEAD_DIM = 64
BLOCK = 128
A_GROUPS = ((128, 1), (512, 4), (2048, 16))
A_HEADS = 8
A_TOTAL_HEADS = len(A_GROUPS) * A_HEADS
B_Q_HEADS = 8
B_KV_HEADS = 2
B_WINDOW = 128
C_HEADS = 8
C_Q_RANK = 256
C_KV_RANK = 128
C_NOPE = 64
C_ROPE = 32
C_V = 64
ROPE_BASE = 10000.0
REL_BUCKETS = 32
REL_MAX_DIST = 2048
N_REL_HEADS = A_TOTAL_HEADS + B_Q_HEADS
N_BRANCH = 3
BRANCH_WIDTH = 512
D_FF = 2816
CONV_WIDTH = 3
ALPHA = (2 * DEPTH) ** 0.25
BETA = (8 * DEPTH) ** -0.25
LN_EPS = 1e-5
RMS_EPS = 1e-6
NEG = -1e30

A_QKV_COLS = len(A_GROUPS) * 3 * A_HEADS * HEAD_DIM
B_Q_COLS = B_Q_HEADS * HEAD_DIM
B_KV_COLS = 2 * B_KV_HEADS * HEAD_DIM
C_DQ_COLS = C_Q_RANK
C_DKV_COLS = C_KV_RANK + C_ROPE
GATE_COLS = N_BRANCH * D_MODEL
IN_SPLITS = (A_QKV_COLS,
             A_QKV_COLS + B_Q_COLS,
             A_QKV_COLS + B_Q_COLS + B_KV_COLS,
             A_QKV_COLS + B_Q_COLS + B_KV_COLS + C_DQ_COLS,
             A_QKV_COLS + B_Q_COLS + B_KV_COLS + C_DQ_COLS + C_DKV_COLS)
D_IN = IN_SPLITS[-1] + GATE_COLS

kernel_name = "hybrid_dilated_swa_mla_convglu_block"


def layer_norm(x, g, b):
    xf = x.astype(jnp.float32)
    mu = xf.mean(-1, keepdims=True)
    var = jnp.square(xf - mu).mean(-1, keepdims=True)
    y = (xf - mu) * lax.rsqrt(var + LN_EPS) * g.astype(jnp.float32) + b.astype(jnp.float32)
    return y.astype(x.dtype)


def rms_norm(x, g):
    xf = x.astype(jnp.float32)
    y = xf * lax.rsqrt(jnp.mean(xf * xf, -1, keepdims=True) + RMS_EPS) * g.astype(jnp.float32)
    return y.astype(x.dtype)


def t5_bucket(dist):
    n = jnp.maximum(dist, 0)
    max_exact = REL_BUCKETS // 2
    scaled = jnp.log(jnp.maximum(n, 1).astype(jnp.float32) / max_exact) / math.log(REL_MAX_DIST / max_exact)
    large = max_exact + (scaled * (REL_BUCKETS - max_exact)).astype(jnp.int32)
    return jnp.where(n < max_exact, n, jnp.minimum(large, REL_BUCKETS - 1))


def apply_rope(x, cos, sin):
    x1, x2 = jnp.split(x, 2, axis=-1)
    c = cos[:, None, :].astype(x.dtype)
    s = sin[:, None, :].astype(x.dtype)
    return jnp.concatenate([x1 * c - x2 * s, x1 * s + x2 * c], axis=-1)


def dilated_group_attention(q, k, v, bias_table, window, dilation):
    b, s, h, dh = q.shape
    w = window // dilation
    span = w * dilation
    s_pad = -(-s // span) * span
    n_sub = s_pad // dilation
    nb = n_sub // w

    def to_blocks(t):
        t = jnp.pad(t, ((0, 0), (0, s_pad - s), (0, 0), (0, 0)))
        t = t.reshape(b, n_sub, dilation, h, dh).transpose(0, 2, 1, 3, 4)
        return t.reshape(b, dilation, nb, w, h, dh)

    def with_prev(t):
        prev = jnp.pad(t[:, :, :-1], ((0, 0), (0, 0), (1, 0), (0, 0), (0, 0), (0, 0)))
        return jnp.concatenate([prev, t], axis=3)

    qb = to_blocks(q)
    kb = with_prev(to_blocks(k))
    vb = with_prev(to_blocks(v))
    logits = jnp.einsum('brnqhd,brnkhd->brnhqk', qb, kb).astype(jnp.float32) * (dh ** -0.5)
    qi = jnp.arange(w)[:, None]
    ki = jnp.arange(2 * w)[None, :]
    step = w + qi - ki
    band = (step >= 0) & (step <= w)
    valid = band[None] & ((jnp.arange(nb)[:, None, None] > 0) | (ki >= w)[None])
    bias = bias_table[t5_bucket(step * dilation)].astype(jnp.float32).transpose(2, 0, 1)
    logits = jnp.where(valid[None, None, :, None], logits + bias, NEG)
    m = logits.max(-1, keepdims=True)
    p = jnp.exp(logits - m)
    l = p.sum(-1, keepdims=True)
    o = jnp.einsum('brnhqk,brnkhd->brnqhd', (p / l).astype(v.dtype), vb)
    lse = (m + jnp.log(l))[..., 0]
    o = o.reshape(b, dilation, n_sub, h, dh).transpose(0, 2, 1, 3, 4).reshape(b, s_pad, h, dh)[:, :s]
    lse = lse.transpose(0, 1, 2, 4, 3).reshape(b, dilation, n_sub, h).transpose(0, 2, 1, 3)
    lse = lse.reshape(b, s_pad, h)[:, :s]
    return o, lse


def sliding_window_sink_attention(q, k, v, sinks, bias_table):
    b, s, hq, dh = q.shape
    hkv = k.shape[2]
    g = hq // hkv
    nb = s // BLOCK
    qb = q.reshape(b, nb, BLOCK, hkv, g, dh)

    def with_prev(t):
        t = t.reshape(b, nb, BLOCK, hkv, dh)
        prev = jnp.pad(t[:, :-1], ((0, 0), (1, 0), (0, 0), (0, 0), (0, 0)))
        return jnp.concatenate([prev, t], axis=2)

    kb = with_prev(k)
    vb = with_prev(v)
    logits = jnp.einsum('bnqhgd,bnchd->bnhgqc', qb, kb).astype(jnp.float32) * (dh ** -0.5)
    qi = jnp.arange(BLOCK)[:, None]
    ci = jnp.arange(2 * BLOCK)[None, :]
    dist = BLOCK + qi - ci
    valid = ((dist >= 0) & (dist < B_WINDOW))[None] & ((jnp.arange(nb)[:, None, None] > 0) | (ci >= BLOCK)[None])
    bias = bias_table[t5_bucket(dist)].astype(jnp.float32).transpose(2, 0, 1).reshape(hkv, g, BLOCK, 2 * BLOCK)
    logits = jnp.where(valid[None, :, None, None], logits + bias, NEG)
    sink = sinks.astype(jnp.float32).reshape(1, 1, hkv, g, 1, 1)
    m = jnp.maximum(logits.max(-1, keepdims=True), sink)
    p = jnp.exp(logits - m)
    denom = p.sum(-1, keepdims=True) + jnp.exp(sink - m)
    o = jnp.einsum('bnhgqc,bnchd->bnqhgd', (p / denom).astype(v.dtype), vb)
    return o.reshape(b, s, hq * dh)


def mla_attention(cq, c_kv, k_rope, q_norm_g, kv_norm_g, w_uq, w_ukv, cos, sin):
    b, s, _ = cq.shape
    q = (rms_norm(cq, q_norm_g) @ w_uq).reshape(b, s, C_HEADS, C_NOPE + C_ROPE)
    q_nope = q[..., :C_NOPE]
    q_rope = apply_rope(q[..., C_NOPE:], cos, sin)
    kv = (rms_norm(c_kv, kv_norm_g) @ w_ukv).reshape(b, s, C_HEADS, C_NOPE + C_V)
    k_nope = kv[..., :C_NOPE]
    v = kv[..., C_NOPE:]
    k_r = apply_rope(k_rope[:, :, None, :], cos, sin)[:, :, 0]
    nb = s // BLOCK
    qn_b = q_nope.reshape(b, nb, BLOCK, C_HEADS, C_NOPE).swapaxes(0, 1)
    qr_b = q_rope.reshape(b, nb, BLOCK, C_HEADS, C_ROPE).swapaxes(0, 1)
    kpos = jnp.arange(s)
    scale = (C_NOPE + C_ROPE) ** -0.5

    def one_block(args):
        i, qn, qr = args
        logits = (jnp.einsum('bqhd,bkhd->bhqk', qn, k_nope)
                  + jnp.einsum('bqhr,bkr->bhqk', qr, k_r)).astype(jnp.float32) * scale
        qpos = i * BLOCK + jnp.arange(BLOCK)
        logits = jnp.where(kpos[None, :] <= qpos[:, None], logits, NEG)
        p = jax.nn.softmax(logits, axis=-1).astype(v.dtype)
        return jnp.einsum('bhqk,bkhd->bqhd', p, v)

    o = lax.map(one_block, (jnp.arange(nb), qn_b, qr_b))
    return o.swapaxes(0, 1).reshape(b, s, C_HEADS * C_V)


def hybrid_mixer(x, w_in, b_gate, sinks, q_norm_g, kv_norm_g, w_uq, w_ukv, w_branch, w_out,
                 rel_table, cos, sin):
    b, s, _ = x.shape
    proj = x @ w_in
    a_qkv, b_q, b_kv, c_q, c_dkv, gate_pre = jnp.split(proj, IN_SPLITS, axis=-1)

    a_qkv = a_qkv.reshape(b, s, len(A_GROUPS), 3, A_HEADS, HEAD_DIM)
    outs, lses = [], []
    for gi, (window, dil) in enumerate(A_GROUPS):
        o_g, lse_g = dilated_group_attention(a_qkv[:, :, gi, 0], a_qkv[:, :, gi, 1], a_qkv[:, :, gi, 2],
                                             rel_table[:, gi * A_HEADS:(gi + 1) * A_HEADS], window, dil)
        outs.append(o_g)
        lses.append(lse_g)
    wts = jax.nn.softmax(jnp.stack(lses, axis=0), axis=0)
    o_a = jnp.einsum('gbsh,gbshd->bshd', wts, jnp.stack(outs, axis=0).astype(jnp.float32))
    o_a = o_a.astype(x.dtype).reshape(b, s, A_HEADS * HEAD_DIM)

    b_k, b_v = jnp.split(b_kv, 2, axis=-1)
    o_b = sliding_window_sink_attention(b_q.reshape(b, s, B_Q_HEADS, HEAD_DIM),
                                        b_k.reshape(b, s, B_KV_HEADS, HEAD_DIM),
                                        b_v.reshape(b, s, B_KV_HEADS, HEAD_DIM),
                                        sinks, rel_table[:, A_TOTAL_HEADS:])

    c_kv, k_rope = jnp.split(c_dkv, [C_KV_RANK], axis=-1)
    o_c = mla_attention(c_q, c_kv, k_rope, q_norm_g, kv_norm_g, w_uq, w_ukv, cos, sin)

    gates = jax.nn.sigmoid((gate_pre + b_gate).astype(jnp.float32)).astype(x.dtype)
    gates = gates.reshape(b, s, N_BRANCH, D_MODEL)
    merged = (gates[:, :, 0] * (o_a @ w_branch[0])
              + gates[:, :, 1] * (o_b @ w_branch[1])
              + gates[:, :, 2] * (o_c @ w_branch[2]))
    return merged @ w_out


def conv_glu_ffn(x, w_up, conv_w, conv_b, w_down):
    u = x @ w_up
    c = u.shape[-1]
    u = lax.conv_general_dilated(u, conv_w[:, None, :].astype(u.dtype), window_strides=(1,),
                                 padding=[(CONV_WIDTH - 1, 0)],
                                 dimension_numbers=('NWC', 'WIO', 'NWC'),
                                 feature_group_count=c) + conv_b
    gate, val = jnp.split(u, 2, axis=-1)
    return (jax.nn.silu(gate) * val) @ w_down


def setup_inputs(seed: int = 0) -> dict:
    key = jax.random.key(seed)
    ks = jax.random.split(key, 20)
    f32 = jnp.float32
    nrm = lambda k, shape, scale: jax.random.normal(k, shape, f32) * scale
    return {
        'x': jax.random.normal(ks[0], (BATCH, SEQ, D_MODEL), f32),
        'rel_table': nrm(ks[1], (REL_BUCKETS, N_REL_HEADS), 0.2),
        'w_in': nrm(ks[2], (DEPTH, D_MODEL, D_IN), D_MODEL ** -0.5),
        'b_gate': nrm(ks[3], (DEPTH, GATE_COLS), 0.01),
        'sinks': nrm(ks[4], (DEPTH, B_Q_HEADS), 0.5),
        'q_norm_g': 1.0 + nrm(ks[5], (DEPTH, C_Q_RANK), 0.05),
        'kv_norm_g': 1.0 + nrm(ks[6], (DEPTH, C_KV_RANK), 0.05),
        'w_uq': nrm(ks[7], (DEPTH, C_Q_RANK, C_HEADS * (C_NOPE + C_ROPE)), C_Q_RANK ** -0.5),
        'w_ukv': nrm(ks[8], (DEPTH, C_KV_RANK, C_HEADS * (C_NOPE + C_V)), C_KV_RANK ** -0.5),
        'w_branch': nrm(ks[9], (DEPTH, N_BRANCH, BRANCH_WIDTH, D_MODEL), BRANCH_WIDTH ** -0.5 * BETA),
        'w_out': nrm(ks[10], (DEPTH, D_MODEL, D_MODEL), D_MODEL ** -0.5 * BETA),
        'ln1_g': 1.0 + nrm(ks[11], (DEPTH, D_MODEL), 0.05),
        'ln1_b': nrm(ks[12], (DEPTH, D_MODEL), 0.01),
        'w_ffn_up': nrm(ks[13], (DEPTH, D_MODEL, 2 * D_FF), D_MODEL ** -0.5),
        'conv_w': nrm(ks[14], (DEPTH, CONV_WIDTH, 2 * D_FF), CONV_WIDTH ** -0.5),
        'conv_b': nrm(ks[15], (DEPTH, 2 * D_FF), 0.01),
        'w_ffn_down': nrm(ks[16], (DEPTH, D_FF, D_MODEL), D_FF ** -0.5 * BETA),
        'ln2_g': 1.0 + nrm(ks[17], (DEPTH, D_MODEL), 0.05),
        'ln2_b': nrm(ks[18], (DEPTH, D_MODEL), 0.01),
    }


def reference(x, rel_table, w_in, b_gate, sinks, q_norm_g, kv_norm_g, w_uq, w_ukv, w_branch, w_out,
              ln1_g, ln1_b, w_ffn_up, conv_w, conv_b, w_ffn_down, ln2_g, ln2_b):
    s = x.shape[1]
    pos = jnp.arange(s, dtype=jnp.float32)
    inv_freq = ROPE_BASE ** (-jnp.arange(0, C_ROPE, 2, dtype=jnp.float32) / C_ROPE)
    ang = pos[:, None] * inv_freq[None, :]
    cos, sin = jnp.cos(ang), jnp.sin(ang)
    for l in range(DEPTH):
        mix = hybrid_mixer(x, w_in[l], b_gate[l], sinks[l], q_norm_g[l], kv_norm_g[l], w_uq[l], w_ukv[l],
                           w_branch[l], w_out[l], rel_table, cos, sin)
        x = layer_norm(ALPHA * x + mix, ln1_g[l], ln1_b[l])
        ff = conv_glu_ffn(x, w_ffn_up[l], conv_w[l], conv_b[l], w_ffn_down[l])
        x = layer_norm(ALPHA * x + ff, ln2_g[l], ln2_b[l])
    return x
```

```python
import contextlib
import numpy as np
import concourse.bass as bass
import concourse.mybir as mybir
from concourse.bass_utils import run_bass_kernel_spmd

F32 = mybir.dt.float32
BF16 = mybir.dt.bfloat16
AF = mybir.ActivationFunctionType
ALU = mybir.AluOpType

D = 1024
SEQ = 8192
NB = 4
DEPTH = 2
OWN = 4096
HIST = 4096
LOC = 8192
DFF = 2816
ALPHA = (2 * DEPTH) ** 0.25
LN_EPS = 1e-5
RMS_EPS = 1e-6
A_DIL = (1, 4, 16)
NCOL_FM = 3840
NCOL_TM = 2080
NCOL_G = 3072
NCOLS = NCOL_FM + NCOL_TM + NCOL_G
GLEN = 384


class Sched:
    def __init__(self, nc, stack, n_dma_sems=8, dma_queues=("sp", "pool")):
        self.nc = nc
        self.eng = {"pe": nc.tensor, "act": nc.scalar, "dve": nc.vector, "pool": nc.gpsimd, "sp": nc.sync}
        self.sem = {}
        self.cnt = {}
        self.stack = stack
        self.epoch = 0
        self.semh = {}
        for e in self.eng:
            self.sem[e] = stack.enter_context(nc.semaphore("s_" + e))
            self.semh[("e", e, 0)] = self.sem[e]
            self.cnt[e] = 0
        self.dsem = {}
        self.dval = {}
        self.drr = {}
        for q in dma_queues:
            self.dsem[q] = [stack.enter_context(nc.semaphore("d_%s%d" % (q, i))) for i in range(n_dma_sems)]
            self.dval[q] = [0] * n_dma_sems
            self.drr[q] = 0
        self.waited = {e: {} for e in self.eng}
        self.lastw = {}
        self.reads = {}
        self.n_inst = 0
        self.n_wait = 0

    def _semh(self, key):
        return self.semh[key] if key[0] == "e" else self.dsem[key[1]][key[2]]

    def _wait(self, e, tok):
        key, val = tok
        if key[0] == "e" and key[1] == e and e == "pe":
            return
        w = self.waited[e]
        if w.get(key, 0) >= val:
            return
        w[key] = val
        self.eng[e].wait_ge(self._semh(key), val)
        self.n_wait += 1

    def _deps(self, e, reads, writes):
        best = {}
        for r in reads:
            t = self.lastw.get(r)
            if t is not None and best.get(t[0], 0) < t[1]:
                best[t[0]] = t[1]
        for w in writes:
            t = self.lastw.get(w)
            if t is not None and best.get(t[0], 0) < t[1]:
                best[t[0]] = t[1]
            for k, v in self.reads.get(w, ()):
                if best.get(k, 0) < v:
                    best[k] = v
        for k, v in best.items():
            self._wait(e, (k, v))

    def _commit(self, tok, reads, writes):
        for w in writes:
            self.lastw[w] = tok
            self.reads[w] = []
        for r in reads:
            if r in writes:
                continue
            lst = self.reads.setdefault(r, [])
            lst.append(tok)
            if len(lst) > 48:
                best = {}
                for k, v in lst:
                    if best.get(k, 0) < v:
                        best[k] = v
                self.reads[r] = list(best.items())

    def op(self, e, fn, reads=(), writes=()):
        self._deps(e, reads, writes)
        ins = fn(self.eng[e])
        self.cnt[e] += 1
        ins.then_inc(self.sem[e], 1)
        tok = (("e", e, self.epoch), self.cnt[e])
        self._commit(tok, reads, writes)
        self.n_inst += 1
        return tok

    def dma(self, q, out, in_, reads=(), writes=(), **kw):
        i = self.drr[q]
        self.drr[q] = (i + 1) % len(self.dsem[q])
        key = ("d", q, i)
        if self.dval[q][i] > 0:
            self._wait(q, (key, self.dval[q][i]))
        self._deps(q, reads, writes)
        ins = self.eng[q].dma_start(out=out, in_=in_, **kw)
        self.dval[q][i] += 16
        ins.then_inc(self.dsem[q][i], 16)
        tok = (key, self.dval[q][i])
        self._commit(tok, reads, writes)
        self.n_inst += 1
        return tok

    def barrier(self):
        for e in self.eng:
            for q in self.dsem:
                for i, v in enumerate(self.dval[q]):
                    if v > 0:
                        self._wait(e, (("d", q, i), v))
            for x in self.eng:
                if x != e and self.cnt[x] > 0:
                    self._wait(e, (("e", x, self.epoch), self.cnt[x]))
        self.lastw = {}
        self.reads = {}
        if max(self.cnt.values()) > 12000:
            self.epoch += 1
            for e in self.eng:
                self.sem[e] = self.stack.enter_context(self.nc.semaphore("s_%s_%d" % (e, self.epoch)))
                self.semh[("e", e, self.epoch)] = self.sem[e]
                self.cnt[e] = 0


class Rot:
    def __init__(self, items):
        self.items = items
        self.i = 0

    def next(self):
        it = self.items[self.i]
        self.i = (self.i + 1) % len(self.items)
        return it


class Ctx:
    pass


_UID = [0]


def _mkps(nc, ph):
    def ps(n, shape, dtp):
        es = 4 if dtp == F32 else 2
        total = 2048 // es
        used = 1
        for v in shape[1:]:
            used *= v
        assert used <= total, (n, shape)
        t = ph.enter_context(nc.psum_tensor(_u(n), [shape[0], total], dtp))
        ap = t[:, 0:used]
        if len(shape) == 3:
            ap = ap.rearrange("p (a b) -> p a b", a=shape[1])
        elif len(shape) == 4:
            ap = ap.rearrange("p (a b c) -> p a b c", a=shape[1], b=shape[2])
        return ap
    return ps


def _u(n):
    _UID[0] += 1
    return "%s_u%d" % (n, _UID[0])


def _alt(i):
    return "dve" if i % 2 == 0 else "act"


def copy_op(S, eng, out, in_, reads, writes):
    if eng == "act":
        return S.op("act", lambda e: e.copy(out=out, in_=in_), reads=reads, writes=writes)
    return S.op(eng, lambda e: e.tensor_copy(out=out, in_=in_), reads=reads, writes=writes)


def setup_consts(C):
    nc, S = C.nc, C.S
    sb = C.sb
    C.identf = sb("identf", [128, 128], F32)
    C.ident = sb("ident", [128, 128], BF16)
    S.op("pool", lambda e: e.memset(C.identf[:], 0.0), writes=["identf"])
    S.op("pool", lambda e: e.affine_select(out=C.identf[:], in_=C.identf[:], pattern=[[-1, 128]],
                                           compare_op=ALU.not_equal, fill=1.0, base=0, channel_multiplier=1),
         reads=["identf"], writes=["identf"])
    S.op("dve", lambda e: e.tensor_copy(out=C.ident[:], in_=C.identf[:]), reads=["identf"], writes=["ident"])
    C.flag = sb("flag", [128, 1], F32)
    S.dma("sp", C.flag[:], C.d["flag"], writes=["flag"])


def phase_proj(C, hist):
    nc, S, d = C.nc, C.S, C.d
    tok0 = 0 if hist else HIST
    with contextlib.ExitStack() as ph:
        sb = lambda n, s, t: ph.enter_context(nc.sbuf_tensor(_u(n), s, t))
        ps = _mkps(nc, ph)
        xT = sb("xT", [128, 8, 4096], BF16)
        xin = [sb("xin%d" % i, [128, 1024], F32) for i in range(2)]
        xbf = [sb("xbf%d" % i, [128, 1024], BF16) for i in range(2)]
        pT = [ps("pT%d" % i, [128, 8, 128], BF16) for i in range(2)]
        wtm = sb("wtm", [128, 8, NCOL_TM], BF16)
        wfm = [sb("wfm%d" % i, [128, 8, 512], BF16) for i in range(2)]
        pmm = [ps("pmm%d" % i, [128, 512], F32) for i in range(4)]
        stage = [sb("stage%d" % i, [128, 2048], BF16) for i in range(2)]
        vst = [[sb("vst%d_%d" % (k, i), [128, 4, 576], BF16) for i in range(2)] for k in range(3)]
        bvst = [sb("bvst%d" % i, [128, 4, 144], BF16) for i in range(2)]
        cst = [sb("cst%d" % i, [128, 4, 416], F32) for i in range(2)]
        S.dma("pool", wtm[:], d["winp"][:, :, NCOL_FM:NCOL_FM + NCOL_TM].rearrange("k p c -> p k c"), writes=["wtm"])
        for k in range(3):
            for i in range(2):
                v4 = vst[k][i][:].rearrange("p a (h c) -> p (a h) c", c=72)
                if hist:
                    S.op("pool", lambda e, v4=v4: e.tensor_copy(out=v4[:, :, 64:72], in_=C.flag[:].unsqueeze(1).broadcast_to([128, 32, 8])),
                         reads=["flag"], writes=[("vst", k, i)])
                else:
                    S.op("pool", lambda e, v4=v4: e.memset(v4[:, :, 64:72], 1.0), writes=[("vst", k, i)])
        for i in range(2):
            v4 = bvst[i][:].rearrange("p a (h c) -> p (a h) c", c=72)
            if hist:
                S.op("pool", lambda e, v4=v4: e.tensor_copy(out=v4[:, :, 64:72], in_=C.flag[:].unsqueeze(1).broadcast_to([128, 8, 8])),
                     reads=["flag"], writes=[("bvst", i)])
            else:
                S.op("pool", lambda e, v4=v4: e.memset(v4[:, :, 64:72], 1.0), writes=[("bvst", i)])
        for t in range(32):
            i = t % 2
            S.dma("sp", xin[i][:], d["xin"][tok0 + t * 128: tok0 + (t + 1) * 128, :], writes=[("xin", i)])
            if hist:
                S.op("dve", lambda e, i=i: e.tensor_scalar(out=xbf[i][:], in0=xin[i][:], scalar1=C.flag[:, 0:1], scalar2=None, op0=ALU.mult),
                     reads=[("xin", i), "flag"], writes=[("xbf", i)])
            else:
                copy_op(S, "dve", xbf[i][:], xin[i][:], [("xin", i)], [("xbf", i)])
            for kc in range(8):
                S.op("pe", lambda e, i=i, kc=kc: e.transpose(out=pT[i][:, kc, :], in_=xbf[i][:, kc * 128:(kc + 1) * 128], identity=C.ident[:]),
                     reads=[("xbf", i), "ident"], writes=[("pT", i)])
            copy_op(S, "act", xT[:, :, t * 128:(t + 1) * 128], pT[i][:], [("pT", i)], [("xT", t // 4)])
        if not hist:
            for kc in range(8):
                S.dma("sp", d["XT"][kc], xT[:, kc, :], reads=[("xT", j) for j in range(8)], writes=[("XTd", kc)])
        units = []
        for g in range(3):
            units.append((g * 1024, 4, d["AQT"][g], A_DIL[g], True))
            units.append((g * 1024 + 512, 4, d["AKT"][g], A_DIL[g], False))
        units.append((3072, 4, d["BQT"], 1, True))
        units.append((3584, 2, d["BKT"], 1, False))
        ui = 0
        pi = 0
        for (c0, ng, dst, dil, own_only) in units:
            if hist and own_only:
                continue
            wi = ui % 2
            ui += 1
            S.dma("pool", wfm[wi][:, :, 0:ng * 128], d["winp"][:, :, c0:c0 + ng * 128].rearrange("k p c -> p k c"), writes=[("wfm", wi)])
            for gi in range(ng):
                for half in range(2):
                    si = (gi * 2 + half) % 2
                    for m in range(4):
                        tt = half * 4 + m
                        p = pi % 4
                        pi += 1
                        for kc in range(8):
                            S.op("pe", lambda e, p=p, wi=wi, gi=gi, kc=kc, tt=tt: e.matmul(
                                pmm[p][:], lhsT=wfm[wi][:, kc, gi * 128:(gi + 1) * 128], rhs=xT[:, kc, tt * 512:(tt + 1) * 512],
                                start=(kc == 0), stop=(kc == 7)),
                                 reads=[("wfm", wi), ("xT", tt)], writes=[("pmm", p)])
                        if dil == 1:
                            o_ap = stage[si][:, m * 512:(m + 1) * 512]
                            i_ap = pmm[p][:]
                        elif dil == 4:
                            o_ap = stage[si][:, m * 512:(m + 1) * 512].rearrange("p (r j) -> p j r", r=4)
                            i_ap = pmm[p][:].rearrange("p (j r) -> p j r", r=4)
                        else:
                            o_ap = stage[si][:].rearrange("p (r j) -> p j r", r=16)[:, 32 * m:32 * m + 32, :]
                            i_ap = pmm[p][:].rearrange("p (j r) -> p j r", r=16)
                        copy_op(S, _alt(pi), o_ap, i_ap, [("pmm", p)], [("stage", si)])
                    base = (0 if own_only else tok0) + half * 2048
                    S.dma("sp", dst[gi, :, base:base + 2048], stage[si][:], reads=[("stage", si)], writes=[("fmd", c0, gi, half)])
        for t in range(32):
            a = t % 4
            q4 = (t // 4) % 2
            chunks = [(0, 512, "v0"), (512, 512, "v1"), (1024, 512, "v2"), (1536, 128, "bv"), (1664, 416, "c")]
            for ci, (c0, w, kind) in enumerate(chunks):
                p = pi % 4
                pi += 1
                for kc in range(8):
                    S.op("pe", lambda e, p=p, kc=kc, c0=c0, w=w, t=t: e.matmul(
                        pmm[p][:, 0:w], lhsT=xT[:, kc, t * 128:(t + 1) * 128], rhs=wtm[:, kc, c0:c0 + w],
                        start=(kc == 0), stop=(kc == 7)),
                         reads=["wtm", ("xT", t // 4)], writes=[("pmm", p)])
                if kind[0] == "v":
                    k = int(kind[1])
                    o_ap = vst[k][q4][:, a, :].rearrange("p (h c) -> p h c", c=72)[:, :, 0:64]
                    i_ap = pmm[p][:].rearrange("p (h c) -> p h c", c=64)
                    copy_op(S, _alt(pi), o_ap, i_ap, [("pmm", p)], [("vst", k, q4)])
                elif kind == "bv":
                    o_ap = bvst[q4][:, a, :].rearrange("p (h c) -> p h c", c=72)[:, :, 0:64]
                    i_ap = pmm[p][:, 0:128].rearrange("p (h c) -> p h c", c=64)
                    copy_op(S, _alt(pi), o_ap, i_ap, [("pmm", p)], [("bvst", q4)])
                else:
                    copy_op(S, _alt(pi), cst[q4][:, a, :], pmm[p][:, 0:416], [("pmm", p)], [("cst", q4)])
            if a == 3:
                r0 = tok0 + (t - 3) * 128
                for k in range(3):
                    S.dma("sp", d["AV"][k][r0:r0 + 512, :].rearrange("(a p) c -> p a c", p=128), vst[k][q4][:],
                          reads=[("vst", k, q4)], writes=[("AVd", k, t)])
                S.dma("sp", d["BV"][r0:r0 + 512, :].rearrange("(a p) c -> p a c", p=128), bvst[q4][:],
                      reads=[("bvst", q4)], writes=[("BVd", t)])
                S.dma("sp", d["CRAW"][r0:r0 + 512, :].rearrange("(a p) c -> p a c", p=128), cst[q4][:],
                      reads=[("cst", q4)], writes=[("CRAWd", t)])
        S.barrier()


def phase_tables(C):
    nc, S, d = C.nc, C.S, C.d
    with contextlib.ExitStack() as ph:
        sb = lambda n, s, t: ph.enter_context(nc.sbuf_tensor(_u(n), s, t))
        ps = _mkps(nc, ph)
        tab = sb("tab", [33, 32], F32)
        oh = sb("oh", [33, 4, GLEN], F32)
        gsb = sb("gsb", [8, 4, GLEN], F32)
        pg = ps("pg", [8, GLEN], F32)
        S.dma("sp", tab[:], d["tabext"], writes=["tab"])
        S.dma("sp", oh[:], d["ohext"].rearrange("t b s -> b t s"), writes=["oh"])
        for typ in range(4):
            S.op("pe", lambda e, typ=typ: e.matmul(pg[:], lhsT=tab[:, typ * 8:(typ + 1) * 8], rhs=oh[:, typ, :], start=True, stop=True),
                 reads=["tab", "oh"], writes=["pg"])
            S.op("act", lambda e, typ=typ: e.activation(out=gsb[:, typ, :], in_=pg[:], func=AF.Exp), reads=["pg"], writes=[("gsb", typ)])
            for rq in range(4):
                S.dma("sp", d["Gd"][typ * 8:(typ + 1) * 8, rq * 32 * GLEN:(rq + 1) * 32 * GLEN].rearrange("h (r s) -> h r s", s=GLEN),
                      gsb[:, typ, :].unsqueeze(1).broadcast_to([8, 32, GLEN]), reads=[("gsb", typ)], writes=[("Gd", typ)])
            S.dma("sp", d["Gd"][typ * 8:(typ + 1) * 8, 128 * GLEN:129 * (GLEN - 1)], gsb[:, typ, 0:129 * (GLEN - 1) - 128 * GLEN],
                  reads=[("gsb", typ)], writes=[("Gd", typ)])
        S.barrier()


def phase_banded(C, typ):
    nc, S, d = C.nc, C.S, C.d
    isB = typ == 3
    dil = 1 if isB else A_DIL[typ]
    span = 128 * dil
    QT = d["BQT"] if isB else d["AQT"][typ]
    KT = d["BKT"] if isB else d["AKT"][typ]
    V = d["BV"] if isB else d["AV"][typ]
    nkp = 2 if isB else 4
    vw = 144 if isB else 576
    OU = d["OU"][typ]
    with contextlib.ExitStack() as ph:
        sb = lambda n, s, t: ph.enter_context(nc.sbuf_tensor(_u(n), s, t))
        ps = _mkps(nc, ph)
        Ksb = sb("Ksb", [128, nkp, 6144], BF16)
        Qsb = sb("Qsb", [128, 8, 4096], BF16)
        Esb = sb("Esb", [128, 8, 2, 128], F32)
        Vsb = [sb("Vsb%d" % i, [128, 2, vw], BF16) for i in range(3)]
        Pf = [sb("Pf%d" % i, [128, 4, 128], F32) for i in range(2)]
        Pb = [sb("Pb%d" % i, [128, 4, 128], BF16) for i in range(3)]
        OUs = [sb("OUs%d" % i, [128, 8, 72], F32) for i in range(2)]
        pS = [[ps("pS%d_%d" % (i, hh), [128, 2, 128], F32) for hh in range(2)] for i in range(2)]
        pO = [[ps("pO%d_%d" % (i, hf), [128, 4, 72], F32) for hf in range(2)] for i in range(2)]
        for kp in range(nkp):
            S.dma("sp", Ksb[:, kp, :], KT[kp, :, 2048:8192], writes=[("Ksb", kp)])
        q4v = Qsb[:].rearrange("p (hp two) t -> p hp two t", two=2)
        S.op("pool", lambda e: e.memset(q4v[64:128, :, 0, :], 0.0), writes=["QsbZ0"])
        S.op("pool", lambda e: e.memset(q4v[0:64, :, 1, :], 0.0), writes=["QsbZ1"])
        for qp in range(4):
            S.dma("sp", Qsb[0:64, 2 * qp, :], QT[qp, 0:64, :], writes=[("QsbA", qp)])
            S.dma("sp", Qsb[64:128, 2 * qp + 1, :], QT[qp, 64:128, :], writes=[("QsbB", qp)])
        for pc in range(2):
            a0 = 255 - 128 * pc
            for h in range(8):
                src = d["Gd"][typ * 8 + h, :].rearrange("(k s) -> k s", s=GLEN - 1)[0:128, a0:a0 + 128]
                S.dma("sp", Esb[:, h, pc, :], src, reads=[("Gd", typ)], writes=["Esb"])
        cnt = 0
        import os
        lvl = int(os.environ.get("MK_BAND", "3"))
        for vb in range(int(os.environ.get("MK_NVB", "32"))):
            n, r = vb // dil, vb % dil
            if getattr(C, "last_only", False) and n != (32 // dil) - 1:
                continue
            vi = vb % 3
            for pc in range(2):
                nn = HIST // span + n - 1 + pc
                src = V.rearrange("(n j r) c -> n r j c", j=128, r=dil)[nn, r]
                S.dma("sp", Vsb[vi][:, pc, :], src, writes=[("Vsb", vi)])
            oi = vb % 2
            for hp in range(4 if lvl >= 2 else 0):
                si = cnt % 2
                bi = cnt % 3
                cnt += 1
                kp = (hp // 2) if isB else hp
                for hh in range(2):
                    for pc in range(2):
                        kc0 = 2048 + (vb - dil + pc * dil) * 128
                        S.op("pe", lambda e, si=si, hh=hh, pc=pc, kp=kp, kc0=kc0, hp=hp, vb=vb: e.matmul(
                            pS[si][hh][:, pc, :], lhsT=Ksb[:, kp, kc0:kc0 + 128],
                            rhs=Qsb[:, 2 * hp + hh, vb * 128:(vb + 1) * 128], start=True, stop=True),
                             reads=[("Ksb", kp), "QsbZ0", "QsbZ1", ("QsbA", hp), ("QsbB", hp)], writes=[("pS", si, hh)])
                for hh in range(2):
                    S.op("act", lambda e, si=si, hh=hh: e.activation(out=Pf[si][:, hh * 2:(hh + 1) * 2, :], in_=pS[si][hh][:], func=AF.Exp, scale=0.125),
                         reads=[("pS", si, hh)], writes=[("Pf", si)])
                e_ap = Esb[:, 2 * hp:2 * hp + 2, :, :].rearrange("p h c q -> p (h c) q")
                S.op("dve", lambda e, si=si, bi=bi, e_ap=e_ap: e.tensor_tensor(out=Pb[bi][:], in0=Pf[si][:], in1=e_ap, op=ALU.mult),
                     reads=[("Pf", si), "Esb"], writes=[("Pb", bi)])
                for hh in range(2 if lvl >= 3 else 0):
                    h = 2 * hp + hh
                    vh = (h // 4) if isB else h
                    for pc in range(2):
                        S.op("pe", lambda e, oi=oi, h=h, hh=hh, pc=pc, bi=bi, vi=vi, vh=vh: e.matmul(
                            pO[oi][h // 4][:, h % 4, :], lhsT=Pb[bi][:, hh * 2 + pc, :], rhs=Vsb[vi][:, pc, vh * 72:(vh + 1) * 72],
                            start=(pc == 0), stop=(pc == 1)),
                             reads=[("Pb", bi), ("Vsb", vi)], writes=[("pO", oi, h // 4)])
            if lvl < 3:
                continue
            for hf in range(2):
                copy_op(S, "act" if hf else "dve", OUs[oi][:, hf * 4:(hf + 1) * 4, :], pO[oi][hf][:], [("pO", oi, hf)], [("OUs", oi)])
            dst = OU.rearrange("(n j r) c -> n r j c", j=128, r=dil)[n, r]
            S.dma("sp", dst, OUs[oi][:].rearrange("p h c -> p (h c)"), reads=[("OUs", oi)], writes=[("OUd", typ, vb)])
            if os.environ.get("MK_SAFE", "1") == "1":
                S.barrier()
        S.barrier()


def phase_combine(C, srcs, br, use_sink):
    nc, S, d = C.nc, C.S, C.d
    with contextlib.ExitStack() as ph:
        sb = lambda n, s, t: ph.enter_context(nc.sbuf_tensor(_u(n), s, t))
        ps = _mkps(nc, ph)
        ou = [[sb("ou%d_%d" % (k, i), [128, 8, 72], F32) for i in range(2)] for k in range(len(srcs))]
        rl = [sb("rl%d" % i, [128, 8], F32) for i in range(2)]
        ob = [sb("ob%d" % i, [128, 8, 64], BF16) for i in range(2)]
        ost = [sb("ost%d" % i, [128, 4, 512], BF16) for i in range(2)]
        pT = [ps("pTc%d" % i, [128, 4, 128], BF16) for i in range(2)]
        if use_sink:
            snk = sb("snk", [128, 8], F32)
            S.dma("sp", snk[:], d["sinks"].partition_broadcast(128), writes=["snk"])
            S.op("act", lambda e: e.activation(out=snk[:], in_=snk[:], func=AF.Exp), reads=["snk"], writes=["snk"])
        for t in range(32):
            if getattr(C, "last_only", False) and t < 28:
                continue
            i = t % 2
            for k, s in enumerate(srcs):
                S.dma("sp", ou[k][i][:].rearrange("p h c -> p (h c)"), d["OU"][s][t * 128:(t + 1) * 128, :], reads=[("OUall", s)], writes=[("ou", k, i)])
            for k in range(1, len(srcs)):
                S.op("dve", lambda e, k=k, i=i: e.tensor_tensor(out=ou[0][i][:], in0=ou[0][i][:], in1=ou[k][i][:], op=ALU.add),
                     reads=[("ou", 0, i), ("ou", k, i)], writes=[("ou", 0, i)])
            if use_sink:
                S.op("dve", lambda e, i=i: e.tensor_tensor(out=rl[i][:], in0=ou[0][i][:, :, 64], in1=snk[:], op=ALU.add),
                     reads=[("ou", 0, i), "snk"], writes=[("rl", i)])
                S.op("dve", lambda e, i=i: e.reciprocal(out=rl[i][:], in_=rl[i][:]), reads=[("rl", i)], writes=[("rl", i)])
            else:
                S.op("dve", lambda e, i=i: e.reciprocal(out=rl[i][:], in_=ou[0][i][:, :, 64]), reads=[("ou", 0, i)], writes=[("rl", i)])
            for h in range(8):
                S.op("dve", lambda e, i=i, h=h: e.tensor_scalar(out=ob[i][:, h, :], in0=ou[0][i][:, h, 0:64], scalar1=rl[i][:, h:h + 1],
                                                                                     scalar2=None, op0=ALU.mult),
                     reads=[("ou", 0, i), ("rl", i)], writes=[("ob", i)])
            for kc in range(4):
                S.op("pe", lambda e, i=i, kc=kc: e.transpose(out=pT[i][:, kc, :], in_=ob[i][:, 2 * kc:2 * kc + 2, :].rearrange("p h c -> p (h c)"), identity=C.ident[:]),
                     reads=[("ob", i), "ident"], writes=[("pTc", i)])
            so = (t // 4) % 2
            copy_op(S, "act", ost[so][:, :, (t % 4) * 128:(t % 4 + 1) * 128], pT[i][:], [("pTc", i)], [("ost", so)])
            if t % 4 == 3:
                c0 = (t - 3) * 128
                S.dma("sp", d["OT"][br][:, :, c0:c0 + 512].rearrange("k p t -> p k t"), ost[so][:], reads=[("ost", so)], writes=[("OTd", br, t)])
        S.barrier()


def phase_mla_prep(C):
    nc, S, d = C.nc, C.S, C.d
    with contextlib.ExitStack() as ph:
        sb = lambda n, s, t: ph.enter_context(nc.sbuf_tensor(_u(n), s, t))
        ps = _mkps(nc, ph)
        wuq = sb("wuq", [128, 2, 768], BF16)
        wukv = sb("wukv", [128, 1024], BF16)
        gq = sb("gq", [128, 256], F32)
        gkv = sb("gkv", [128, 128], F32)
        S.dma("pool", wuq[:], d["wuq"].rearrange("(k p) c -> p k c", p=128), writes=["wuq"])
        S.dma("pool", wukv[:], d["wukv"], writes=["wukv"])
        S.dma("sp", gq[:], d["qg"].partition_broadcast(128), writes=["gq"])
        S.dma("sp", gkv[:], d["kvg"].partition_broadcast(128), writes=["gkv"])
        craw = [sb("craw%d" % i, [128, 416], F32) for i in range(2)]
        rope = [sb("rope%d" % i, [128, 32], F32) for i in range(2)]
        junk = sb("junk", [128, 256], F32)
        ssq = [sb("ssq%d" % i, [128, 4], F32) for i in range(2)]
        cn = [sb("cn%d" % i, [128, 384], BF16) for i in range(2)]
        cT = [sb("cT%d" % i, [128, 3, 128], BF16) for i in range(2)]
        Kt = [sb("Kt%d" % i, [128, 8, 96], BF16) for i in range(2)]
        Qt = [sb("Qt%d" % i, [128, 8, 96], BF16) for i in range(2)]
        rt = [sb("rt%d" % i, [128, 6, 16], F32) for i in range(2)]
        qr = [sb("qr%d" % i, [128, 4, 8, 16], F32) for i in range(2)]
        Vst = [sb("Vst%d" % i, [128, 4, 576], BF16) for i in range(2)]
        KTs = [sb("KTs%d" % i, [96, 8, 512], BF16) for i in range(2)]
        QTs = [sb("QTs%d" % i, [96, 8, 512], BF16) for i in range(2)]
        pTc = [ps("pTm%d" % i, [128, 3, 128], BF16) for i in range(1)]
        pQa = ps("pQa", [128, 480], F32)
        pQb = ps("pQb", [128, 288], F32)
        pKV = [ps("pKV%d" % i, [128, 512], F32) for i in range(2)]
        pKT = [ps("pKT%d" % i, [96, 8, 128], BF16) for i in range(2)]
        for i in range(2):
            v4 = Vst[i][:].rearrange("p a (h c) -> p (a h) c", c=72)
            S.op("pool", lambda e, v4=v4: e.memset(v4[:, :, 64:72], 1.0), writes=[("Vst", i)])
        flagset = False
        for t in range(64):
            i = t % 2
            own = t >= 32
            a = t % 4
            q4 = (t // 4) % 2
            if t == 0:
                for j in range(2):
                    v4 = Vst[j][:].rearrange("p a (h c) -> p (a h) c", c=72)
                    S.op("pool", lambda e, v4=v4: e.tensor_copy(out=v4[:, :, 64:72], in_=C.flag[:].unsqueeze(1).broadcast_to([128, 32, 8])),
                         reads=["flag"], writes=[("Vst", j)])
            if t == 32:
                for j in range(2):
                    v4 = Vst[j][:].rearrange("p a (h c) -> p (a h) c", c=72)
                    S.op("pool", lambda e, v4=v4: e.memset(v4[:, :, 64:72], 1.0), reads=[], writes=[("Vst", j)])
            S.dma("sp", craw[i][:], d["CRAW"][t * 128:(t + 1) * 128, :], reads=[("CRAWall",)], writes=[("craw", i)])
            S.dma("sp", rope[i][:], d["rope"][t * 128:(t + 1) * 128, :], writes=[("rope", i)])
            S.op("act", lambda e, i=i: e.activation(out=junk[:, 0:256], in_=craw[i][:, 0:256], func=AF.Square, accum_out=ssq[i][:, 0:1]),
                 reads=[("craw", i)], writes=["junk", ("ssq", i)])
            S.op("act", lambda e, i=i: e.activation(out=junk[:, 0:128], in_=craw[i][:, 256:384], func=AF.Square, accum_out=ssq[i][:, 1:2]),
                 reads=[("craw", i)], writes=["junk", ("ssq", i)])
            S.op("act", lambda e, i=i: e.activation(out=ssq[i][:, 2:3], in_=ssq[i][:, 0:1], func=AF.Sqrt, bias=RMS_EPS, scale=1.0 / 256),
                 reads=[("ssq", i)], writes=[("ssq", i)])
            S.op("act", lambda e, i=i: e.activation(out=ssq[i][:, 3:4], in_=ssq[i][:, 1:2], func=AF.Sqrt, bias=RMS_EPS, scale=1.0 / 128),
                 reads=[("ssq", i)], writes=[("ssq", i)])
            S.op("dve", lambda e, i=i: e.reciprocal(out=ssq[i][:, 2:4], in_=ssq[i][:, 2:4]), reads=[("ssq", i)], writes=[("ssq", i)])
            S.op("dve", lambda e, i=i: e.scalar_tensor_tensor(out=cn[i][:, 0:256], in0=craw[i][:, 0:256], scalar=ssq[i][:, 2:3], in1=gq[:],
                                                              op0=ALU.mult, op1=ALU.mult),
                 reads=[("craw", i), ("ssq", i), "gq"], writes=[("cn", i)])
            S.op("dve", lambda e, i=i: e.scalar_tensor_tensor(out=cn[i][:, 256:384], in0=craw[i][:, 256:384], scalar=ssq[i][:, 3:4], in1=gkv[:],
                                                               op0=ALU.mult, op1=ALU.mult),
                 reads=[("craw", i), ("ssq", i), "gkv"], writes=[("cn", i)])
            for kc in range(3):
                S.op("pe", lambda e, i=i, kc=kc: e.transpose(out=pTc[0][:, kc, :], in_=cn[i][:, kc * 128:(kc + 1) * 128], identity=C.ident[:]),
                     reads=[("cn", i), "ident"], writes=["pTm"])
            copy_op(S, "act", cT[i][:], pTc[0][:], ["pTm"], [("cT", i)])
            x1 = craw[i][:, 384:400]
            x2 = craw[i][:, 400:416]
            cs = rope[i][:, 0:16]
            sn = rope[i][:, 16:32]
            R = rt[i]
            rd = [("craw", i), ("rope", i)]
            S.op("pool", lambda e, R=R, x1=x1, cs=cs: e.tensor_tensor(out=R[:, 0, :], in0=x1, in1=cs, op=ALU.mult), reads=rd, writes=[("rt", i)])
            S.op("pool", lambda e, R=R, x2=x2, sn=sn: e.tensor_tensor(out=R[:, 1, :], in0=x2, in1=sn, op=ALU.mult), reads=rd, writes=[("rt", i)])
            S.op("pool", lambda e, R=R, x1=x1, sn=sn: e.tensor_tensor(out=R[:, 2, :], in0=x1, in1=sn, op=ALU.mult), reads=rd, writes=[("rt", i)])
            S.op("pool", lambda e, R=R, x2=x2, cs=cs: e.tensor_tensor(out=R[:, 3, :], in0=x2, in1=cs, op=ALU.mult), reads=rd, writes=[("rt", i)])
            S.op("pool", lambda e, R=R: e.tensor_tensor(out=R[:, 4, :], in0=R[:, 0, :], in1=R[:, 1, :], op=ALU.subtract), reads=[("rt", i)], writes=[("rt", i)])
            S.op("pool", lambda e, R=R: e.tensor_tensor(out=R[:, 5, :], in0=R[:, 2, :], in1=R[:, 3, :], op=ALU.add), reads=[("rt", i)], writes=[("rt", i)])
            for h in range(8):
                S.op("pool" if h % 2 else "dve", lambda e, i=i, h=h, R=R: e.tensor_copy(out=Kt[i][:, h, 64:96], in_=R[:, 4:6, :].rearrange("p a c -> p (a c)")),
                     reads=[("rt", i)], writes=[("Kt", i)])
            for c in range(2):
                S.op("pe", lambda e, i=i, c=c: e.matmul(pKV[c][:], lhsT=cT[i][:, 2, :], rhs=wukv[:, c * 512:(c + 1) * 512], start=True, stop=True),
                     reads=[("cT", i), "wukv"], writes=[("pKV", c)])
                kv3 = pKV[c][:].rearrange("p (h c) -> p h c", c=128)
                copy_op(S, "dve", Kt[i][:, c * 4:(c + 1) * 4, 0:64], kv3[:, :, 0:64], [("pKV", c)], [("Kt", i)])
                vo = Vst[q4][:, a, :].rearrange("p (h c) -> p h c", c=72)[:, c * 4:(c + 1) * 4, 0:64]
                copy_op(S, "dve", vo, kv3[:, :, 64:128], [("pKV", c)], [("Vst", q4)])
            ki = t % 2
            for h in range(8):
                S.op("pe", lambda e, i=i, h=h, ki=ki: e.transpose(out=pKT[ki][:, h, :], in_=Kt[i][:, h, :], identity=C.ident[:]),
                     reads=[("Kt", i), "ident"], writes=[("pKT", ki)])
            copy_op(S, "act", KTs[q4][:, :, a * 128:(a + 1) * 128], pKT[ki][:], [("pKT", ki)], [("KTs", q4)])
            if own:
                for (pq, c0, nh) in ((pQa, 0, 5), (pQb, 480, 3)):
                    for kc in range(2):
                        S.op("pe", lambda e, i=i, kc=kc, pq=pq, c0=c0, nh=nh: e.matmul(pq[:], lhsT=cT[i][:, kc, :], rhs=wuq[:, kc, c0:c0 + nh * 96],
                                                                                        start=(kc == 0), stop=(kc == 1)),
                             reads=[("cT", i), "wuq"], writes=[("pQ", c0)])
                    q3 = pq[:].rearrange("p (h c) -> p h c", c=96)
                    h0 = c0 // 96
                    copy_op(S, "dve", Qt[i][:, h0:h0 + nh, 0:64], q3[:, :, 0:64], [("pQ", c0)], [("Qt", i)])
                    Q = qr[i]
                    csb = cs.unsqueeze(1).broadcast_to([128, nh, 16])
                    snb = sn.unsqueeze(1).broadcast_to([128, nh, 16])
                    qx1 = q3[:, :, 64:80]
                    qx2 = q3[:, :, 80:96]
                    hs = slice(h0, h0 + nh)
                    rdq = [("pQ", c0), ("rope", i)]
                    S.op("dve", lambda e, Q=Q, qx1=qx1, csb=csb, hs=hs: e.tensor_tensor(out=Q[:, 0, hs, :], in0=qx1, in1=csb, op=ALU.mult), reads=rdq, writes=[("qr", i, c0)])
                    S.op("dve", lambda e, Q=Q, qx2=qx2, snb=snb, hs=hs: e.tensor_tensor(out=Q[:, 1, hs, :], in0=qx2, in1=snb, op=ALU.mult), reads=rdq, writes=[("qr", i, c0)])
                    S.op("dve", lambda e, Q=Q, qx1=qx1, snb=snb, hs=hs: e.tensor_tensor(out=Q[:, 2, hs, :], in0=qx1, in1=snb, op=ALU.mult), reads=rdq, writes=[("qr", i, c0)])
                    S.op("dve", lambda e, Q=Q, qx2=qx2, csb=csb, hs=hs: e.tensor_tensor(out=Q[:, 3, hs, :], in0=qx2, in1=csb, op=ALU.mult), reads=rdq, writes=[("qr", i, c0)])
                    S.op("pool", lambda e, Q=Q, hs=hs, i=i: e.tensor_tensor(out=Qt[i][:, hs, 64:80], in0=Q[:, 0, hs, :], in1=Q[:, 1, hs, :], op=ALU.subtract),
                         reads=[("qr", i, c0)], writes=[("Qt", i)])
                    S.op("pool", lambda e, Q=Q, hs=hs, i=i: e.tensor_tensor(out=Qt[i][:, hs, 80:96], in0=Q[:, 2, hs, :], in1=Q[:, 3, hs, :], op=ALU.add),
                         reads=[("qr", i, c0)], writes=[("Qt", i)])
                ki2 = (t + 1) % 2
                for h in range(8):
                    S.op("pe", lambda e, i=i, h=h, ki2=ki2: e.transpose(out=pKT[ki2][:, h, :], in_=Qt[i][:, h, :], identity=C.ident[:]),
                         reads=[("Qt", i), "ident"], writes=[("pKT", ki2)])
                copy_op(S, "dve", QTs[q4][:, :, a * 128:(a + 1) * 128], pKT[ki2][:], [("pKT", ki2)], [("QTs", q4)])
            if a == 3:
                r0 = (t - 3) * 128
                S.dma("sp", d["CV"][r0:r0 + 512, :].rearrange("(a p) c -> p a c", p=128), Vst[q4][:], reads=[("Vst", q4)], writes=[("CVd", t)])
                S.dma("sp", d["CKT"][:, :, r0:r0 + 512].rearrange("h p t -> p h t"), KTs[q4][:], reads=[("KTs", q4)], writes=[("CKTd", t)])
                if own:
                    S.dma("sp", d["CQT"][:, :, r0 - HIST:r0 - HIST + 512].rearrange("h p t -> p h t"), QTs[q4][:], reads=[("QTs", q4)], writes=[("CQTd", t)])
        S.barrier()


def phase_mla(C):
    import os
    nc, S, d = C.nc, C.S, C.d
    scale = 96.0 ** -0.5
    with contextlib.ExitStack() as ph:
        sb = lambda n, s, t: ph.enter_context(nc.sbuf_tensor(_u(n), s, t))
        ps = _mkps(nc, ph)
        Ksb = sb("mK", [96, 4, 8192], BF16)
        Qsb = sb("mQ", [96, 4, 4096], BF16)
        Vsb = sb("mV", [128, 64, 288], BF16)
        msk = sb("msk", [128, 4, 512], BF16)
        Pb = [sb("mPb%d" % i, [128, 512], BF16) for i in range(4)]
        Pm = [sb("mPm%d" % i, [128, 512], BF16) for i in range(2)]
        Osb = [sb("mOs%d" % i, [72, 512], F32) for i in range(2)]
        OUst = [sb("mOU%d" % i, [128, 4, 4, 72], F32) for i in range(2)]
        pS = [ps("mpS%d" % i, [128, 512], F32) for i in range(3)]
        pO = [ps("mpO%d" % i, [72, 512], F32) for i in range(4)]
        pOT = ps("mpOT", [128, 4, 72], F32)
        S.op("pool", lambda e: e.memset(msk[:], 1.0), writes=["msk"])
        for o in range(4):
            S.op("pool", lambda e, o=o: e.affine_select(out=msk[:, o, :], in_=msk[:, o, :], pattern=[[1, 512]], compare_op=ALU.is_ge, fill=0.0,
                                                        base=-128 * o, channel_multiplier=-1), reads=["msk"], writes=["msk"])
        cnt = 0
        for hg in range(2):
            for hh in range(4):
                S.dma("sp", Ksb[:, hh, :], d["CKT"][hg * 4 + hh], reads=["mKfree"], writes=["mK"])
                S.dma("sp", Qsb[:, hh, :], d["CQT"][hg * 4 + hh], reads=["mKfree"], writes=["mQ"])
            for kq in range(4):
                S.dma("sp", Vsb[:, kq * 16:(kq + 1) * 16, :],
                      d["CV"][kq * 2048:(kq + 1) * 2048, hg * 288:(hg + 1) * 288].rearrange("(a p) c -> p a c", p=128),
                      reads=["mKfree"], writes=["mV"])
            for j in range(8):
                if getattr(C, "last_only", False) and j < 7:
                    continue
                nkt = 32 + 4 * (j + 1)
                for kt in range(nkt):
                    o = kt - 32 - 4 * j
                    for hh in range(4):
                        si = cnt % 3
                        bi = cnt % 4
                        mi = cnt % 2
                        cnt += 1
                        S.op("pe", lambda e, si=si, hh=hh, kt=kt, j=j: e.matmul(pS[si][:], lhsT=Ksb[:, hh, kt * 128:(kt + 1) * 128],
                                                                                 rhs=Qsb[:, hh, j * 512:(j + 1) * 512], start=True, stop=True),
                             reads=["mK", "mQ"], writes=[("mpS", si)])
                        S.op("act", lambda e, si=si, bi=bi: e.activation(out=Pb[bi][:], in_=pS[si][:], func=AF.Exp, scale=scale),
                             reads=[("mpS", si)], writes=[("mPb", bi)])
                        if o >= 0:
                            S.op("dve", lambda e, bi=bi, mi=mi, o=o: e.tensor_tensor(out=Pm[mi][:], in0=Pb[bi][:], in1=msk[:, o, :], op=ALU.mult),
                                 reads=[("mPb", bi), "msk"], writes=[("mPm", mi)])
                            rhs, rk = Pm[mi], ("mPm", mi)
                        else:
                            rhs, rk = Pb[bi], ("mPb", bi)
                        S.op("pe", lambda e, hh=hh, kt=kt, rhs=rhs, nkt=nkt: e.matmul(pO[hh][:], lhsT=Vsb[:, kt, hh * 72:(hh + 1) * 72], rhs=rhs[:],
                                                                                       start=(kt == 0), stop=(kt == nkt - 1)),
                             reads=["mV", rk], writes=[("mpO", hh)])
                ui = j % 2
                for hh in range(4):
                    oi = hh % 2
                    copy_op(S, "dve", Osb[oi][:], pO[hh][:], [("mpO", hh)], [("mOs", oi)])
                    for sub in range(4):
                        S.op("pe", lambda e, oi=oi, sub=sub: e.transpose(out=pOT[:, sub, :], in_=Osb[oi][:, sub * 128:(sub + 1) * 128], identity=C.identf[0:72, 0:72]),
                             reads=[("mOs", oi), "identf"], writes=["mpOT"])
                    copy_op(S, "dve", OUst[ui][:, :, hh, :], pOT[:], ["mpOT"], [("mOU", ui)])
                S.dma("sp", d["OU"][4][j * 512:(j + 1) * 512, hg * 288:(hg + 1) * 288].rearrange("(a p) c -> p a c", p=128),
                      OUst[ui][:].rearrange("p a h c -> p a (h c)"), reads=[("mOU", ui)], writes=[("OUd", 4, hg, j)])
                if os.environ.get("MK_SAFE", "1") == "1":
                    S.barrier()
            S.op("pe", lambda e: e.matmul(pS[0][:, 0:8], lhsT=Ksb[:, 0, 0:128], rhs=Qsb[:, 0, 0:8], start=True, stop=True),
                 reads=["mK", "mQ", "mV"], writes=["mKfree", ("mpS", 0)])
        S.barrier()


def ln_tile(S, C, ph_sb, y, out, gbc, bbc, key, i):
    st6, mv = ph_sb["st6"][i], ph_sb["mv"][i]
    for c in range(2):
        S.op("dve", lambda e, c=c: e.bn_stats(out=st6[:, c, :], in_=y[:, c * 512:(c + 1) * 512]), reads=[key], writes=[("st6", i)])
    S.op("dve", lambda e: e.bn_aggr(out=mv[:, 0:2], in_=st6[:].rearrange("p a c -> p (a c)")), reads=[("st6", i)], writes=[("mv", i)])
    S.op("act", lambda e: e.activation(out=mv[:, 2:3], in_=mv[:, 1:2], func=AF.Sqrt, bias=LN_EPS, scale=1.0), reads=[("mv", i)], writes=[("mv", i)])
    S.op("dve", lambda e: e.reciprocal(out=mv[:, 2:3], in_=mv[:, 2:3]), reads=[("mv", i)], writes=[("mv", i)])
    S.op("dve", lambda e: e.tensor_scalar(out=y, in0=y, scalar1=mv[:, 0:1], scalar2=mv[:, 2:3], op0=ALU.subtract, op1=ALU.mult),
         reads=[key, ("mv", i)], writes=[key])
    S.op("pool", lambda e: e.tensor_tensor(out=y, in0=y, in1=gbc[:], op=ALU.mult), reads=[key, "lng"], writes=[key])
    S.op("pool", lambda e: e.tensor_tensor(out=out, in0=y, in1=bbc[:], op=ALU.add), reads=[key, "lnb"], writes=[key, key + ("o",)])


def phase_merge(C):
    nc, S, d = C.nc, C.S, C.d
    with contextlib.ExitStack() as ph:
        sb = lambda n, s, t: ph.enter_context(nc.sbuf_tensor(_u(n), s, t))
        ps = _mkps(nc, ph)
        wg = sb("wg", [128, 8, 3072], BF16)
        wbr = sb("wbr", [128, 3, 4, 1024], BF16)
        wout = sb("wout", [128, 8, 1024], BF16)
        bg = sb("bg", [128, 3072], F32)
        lng = sb("lng", [128, 1024], F32)
        lnb = sb("lnb", [128, 1024], F32)
        for h2 in range(2):
            S.dma("pool", wg[:, :, h2 * 1536:(h2 + 1) * 1536],
                  d["winp"][:, :, NCOL_FM + NCOL_TM + h2 * 1536:NCOL_FM + NCOL_TM + (h2 + 1) * 1536].rearrange("k p c -> p k c"), writes=["wg"])
        for br in range(3):
            S.dma("pool", wbr[:, br, :, :], d["wbr"][br].rearrange("(k p) c -> p k c", p=128), writes=["wbr"])
        S.dma("pool", wout[:], d["wout"].rearrange("(k p) c -> p k c", p=128), writes=["wout"])
        S.dma("sp", bg[:], d["bgate"].partition_broadcast(128), writes=["bg"])
        S.dma("sp", lng[:], d["ln1g"].partition_broadcast(128), writes=["lng"])
        S.dma("sp", lnb[:], d["ln1b"].partition_broadcast(128), writes=["lnb"])
        xTs = [sb("xTs%d" % i, [128, 8, 512], BF16) for i in range(2)]
        OTs = [[sb("OTs%d_%d" % (br, i), [128, 4, 512], BF16) for i in range(2)] for br in range(3)]
        xt = [sb("xt%d" % i, [128, 1024], F32) for i in range(2)]
        gs = [sb("gs%d" % i, [128, 512], F32) for i in range(2)]
        tmp = [sb("tmp%d" % i, [128, 512], F32) for i in range(2)]
        mg = [sb("mg%d" % i, [128, 1024], F32) for i in range(2)]
        mgb = [sb("mgb%d" % i, [128, 1024], BF16) for i in range(2)]
        mT = [sb("mT%d" % i, [128, 8, 128], BF16) for i in range(2)]
        y = [sb("y%d" % i, [128, 1024], F32) for i in range(2)]
        yo = [sb("yo%d" % i, [128, 1024], F32) for i in range(2)]
        stat = {"st6": [sb("st6_%d" % i, [128, 2, 6], F32) for i in range(2)], "mv": [sb("mv%d" % i, [128, 4], F32) for i in range(2)]}
        pG = [ps("pG%d" % i, [128, 512], F32) for i in range(2)]
        pB = [ps("pB%d" % i, [128, 512], F32) for i in range(2)]
        pT = ps("pTg", [128, 8, 128], BF16)
        pM = [ps("pM%d" % i, [128, 512], F32) for i in range(2)]
        cnt = 0
        for st in range(8):
            if getattr(C, "last_only", False) and st < 7:
                continue
            s2 = st % 2
            S.dma("sp", xTs[s2][:], d["XT"][:, :, st * 512:(st + 1) * 512].rearrange("k p t -> p k t"), reads=[("XTall",)], writes=[("xTs", s2)])
            for br in range(3):
                S.dma("sp", OTs[br][s2][:], d["OT"][br][:, :, st * 512:(st + 1) * 512].rearrange("k p t -> p k t"), reads=[("OTall",)], writes=[("OTs", br, s2)])
            for sub in range(4):
                t = st * 4 + sub
                if getattr(C, "last_only", False) and t < 31:
                    continue
                i = t % 2
                S.dma("sp", xt[i][:], d["xin"][HIST + t * 128:HIST + (t + 1) * 128, :], writes=[("xt", i)])
                for br in range(3):
                    for half in range(2):
                        c = cnt % 2
                        cnt += 1
                        for kc in range(8):
                            S.op("pe", lambda e, c=c, kc=kc, s2=s2, sub=sub, br=br, half=half: e.matmul(
                                pG[c][:], lhsT=xTs[s2][:, kc, sub * 128:(sub + 1) * 128], rhs=wg[:, kc, br * 1024 + half * 512: br * 1024 + (half + 1) * 512],
                                start=(kc == 0), stop=(kc == 7)), reads=[("xTs", s2), "wg"], writes=[("pG", c)])
                        for kc in range(4):
                            S.op("pe", lambda e, c=c, kc=kc, s2=s2, sub=sub, br=br, half=half: e.matmul(
                                pB[c][:], lhsT=OTs[br][s2][:, kc, sub * 128:(sub + 1) * 128], rhs=wbr[:, br, kc, half * 512:(half + 1) * 512],
                                start=(kc == 0), stop=(kc == 3)), reads=[("OTs", br, s2), "wbr"], writes=[("pB", c)])
                        bcols = bg[:, br * 1024 + half * 512: br * 1024 + (half + 1) * 512]
                        S.op("dve", lambda e, c=c, bcols=bcols: e.tensor_tensor(out=gs[c][:], in0=pG[c][:], in1=bcols, op=ALU.add),
                             reads=[("pG", c), "bg"], writes=[("gs", c)])
                        S.op("act", lambda e, c=c: e.activation(out=gs[c][:], in_=gs[c][:], func=AF.Sigmoid), reads=[("gs", c)], writes=[("gs", c)])
                        mslice = mg[i][:, half * 512:(half + 1) * 512]
                        if br == 0:
                            S.op("dve", lambda e, c=c, mslice=mslice: e.tensor_tensor(out=mslice, in0=gs[c][:], in1=pB[c][:], op=ALU.mult),
                                 reads=[("gs", c), ("pB", c)], writes=[("mg", i, half)])
                        else:
                            S.op("dve", lambda e, c=c: e.tensor_tensor(out=tmp[c][:], in0=gs[c][:], in1=pB[c][:], op=ALU.mult),
                                 reads=[("gs", c), ("pB", c)], writes=[("tmp", c)])
                            if br == 1:
                                S.op("pool", lambda e, c=c, mslice=mslice: e.tensor_tensor(out=mslice, in0=mslice, in1=tmp[c][:], op=ALU.add),
                                     reads=[("tmp", c), ("mg", i, half)], writes=[("mg", i, half)])
                            else:
                                S.op("pool", lambda e, c=c, mslice=mslice, i=i, half=half: e.tensor_tensor(out=mgb[i][:, half * 512:(half + 1) * 512], in0=mslice, in1=tmp[c][:], op=ALU.add),
                                     reads=[("tmp", c), ("mg", i, half)], writes=[("mgb", i)])
                for kc in range(8):
                    S.op("pe", lambda e, i=i, kc=kc: e.transpose(out=pT[:, kc, :], in_=mgb[i][:, kc * 128:(kc + 1) * 128], identity=C.ident[:]),
                         reads=[("mgb", i), "ident"], writes=["pTg"])
                copy_op(S, "act", mT[i][:], pT[:], ["pTg"], [("mT", i)])
                for half in range(2):
                    for kc in range(8):
                        S.op("pe", lambda e, i=i, kc=kc, half=half: e.matmul(pM[half][:], lhsT=mT[i][:, kc, :], rhs=wout[:, kc, half * 512:(half + 1) * 512],
                                                                            start=(kc == 0), stop=(kc == 7)), reads=[("mT", i), "wout"], writes=[("pM", half)])
                    S.op("dve", lambda e, i=i, half=half: e.scalar_tensor_tensor(out=y[i][:, half * 512:(half + 1) * 512], in0=xt[i][:, half * 512:(half + 1) * 512],
                                                                                 scalar=ALPHA, in1=pM[half][:], op0=ALU.mult, op1=ALU.add),
                         reads=[("xt", i), ("pM", half)], writes=[("y", i)])
                ln_tile(S, C, stat, y[i][:], yo[i][:], lng, lnb, ("y", i), i)
                S.dma("sp", d["xout"][t * 128:(t + 1) * 128, :], yo[i][:], reads=[("y", i, "o"), ("y", i)], writes=[("xoutd", t)])
        S.barrier()


def phase_ffn(C):
    nc, S, d = C.nc, C.S, C.d
    with contextlib.ExitStack() as ph:
        sb = lambda n, s, t: ph.enter_context(nc.sbuf_tensor(_u(n), s, t))
        ps = _mkps(nc, ph)
        wup = sb("wup", [128, 8, 2 * DFF], BF16)
        wdn = sb("wdn", [128, 22, 1024], BF16)
        cw = sb("cw", [128, 44, 3], F32)
        cb = sb("cb", [128, 44], F32)
        lng = sb("lng2", [128, 1024], F32)
        lnb = sb("lnb2", [128, 1024], F32)
        for q in range(4):
            S.dma("pool", wup[:, :, q * 1408:(q + 1) * 1408], d["wup"][:, q * 1408:(q + 1) * 1408].rearrange("(k p) c -> p k c", p=128), writes=["wup"])
        for q in range(2):
            S.dma("pool", wdn[:, q * 11:(q + 1) * 11, :], d["wdn"][q * 1408:(q + 1) * 1408, :].rearrange("(k p) c -> p k c", p=128), writes=["wdn"])
        S.dma("sp", cw[:], d["convw"], writes=["cw"])
        S.dma("sp", cb[:], d["convb"], writes=["cb"])
        S.dma("sp", lng[:], d["ln2g"].partition_broadcast(128), writes=["lng"])
        S.dma("sp", lnb[:], d["ln2b"].partition_broadcast(128), writes=["lnb"])
        xt = [sb("fx%d" % i, [128, 1024], F32) for i in range(1)]
        xb = [sb("fxb%d" % i, [128, 1024], BF16) for i in range(2)]
        xTs = [sb("fxT%d" % i, [128, 8, 512], BF16) for i in range(1)]
        xhT = sb("fxhT", [128, 8, 128], BF16)
        hal = sb("hal", [128, 44, 2], F32)
        u = [sb("fu%d" % i, [128, 514], F32) for i in range(2)]
        ca = [sb("fca%d" % i, [128, 512], F32) for i in range(4)]
        hT = [sb("fhT%d" % i, [128, 22, 512], BF16) for i in range(1)]
        y = [sb("fy%d" % i, [128, 1024], F32) for i in range(2)]
        yo = y
        stat = {"st6": [sb("fst6_%d" % i, [128, 2, 6], F32) for i in range(2)], "mv": [sb("fmv%d" % i, [128, 4], F32) for i in range(2)]}
        pT = [ps("fpT%d" % i, [128, 8, 128], BF16) for i in range(2)]
        pU = [ps("fpU%d" % i, [128, 512], F32) for i in range(4)]
        pD = [ps("fpD%d" % i, [128, 512], F32) for i in range(2)]
        S.dma("sp", xt[0][:], d["xhalo"], writes=[("fx", 0)])
        S.op("dve", lambda e: e.tensor_scalar(out=xb[0][:], in0=xt[0][:], scalar1=C.flag[:, 0:1], scalar2=None, op0=ALU.mult),
             reads=[("fx", 0), "flag"], writes=[("fxb", 0)])
        for kc in range(8):
            S.op("pe", lambda e, kc=kc: e.transpose(out=pT[0][:, kc, :], in_=xb[0][:, kc * 128:(kc + 1) * 128], identity=C.ident[:]),
                 reads=[("fxb", 0), "ident"], writes=[("fpT", 0)])
        copy_op(S, "act", xhT[:], pT[0][:], [("fpT", 0)], ["fxhT"])
        ph_ = pU[0][:, 0:176].rearrange("p (c j) -> p c j", j=4)[:, :, 2:4]
        for c in range(44):
            for kc in range(8):
                S.op("pe", lambda e, c=c, kc=kc: e.matmul(pU[0][:, 4 * c:4 * c + 4], lhsT=wup[:, kc, c * 128:(c + 1) * 128], rhs=xhT[:, kc, 124:128],
                                                          start=(kc == 0), stop=(kc == 7)), reads=["wup", "fxhT"], writes=[("fpU", 0)])
        copy_op(S, "dve", hal[:], ph_, [("fpU", 0)], ["hal"])
        cnt = 0
        ucnt = 0
        for st in range(8):
            s2 = 0
            for sub in range(4):
                t = st * 4 + sub
                bi = t % 2
                S.dma("pool", xb[bi][:], d["x1in"][t * 128:(t + 1) * 128, :], writes=[("fxb", bi)])
                for kc in range(8):
                    S.op("pe", lambda e, bi=bi, kc=kc: e.transpose(out=pT[bi][:, kc, :], in_=xb[bi][:, kc * 128:(kc + 1) * 128], identity=C.ident[:]),
                         reads=[("fxb", bi), "ident"], writes=[("fpT", bi)])
                copy_op(S, "act", xTs[s2][:, :, sub * 128:(sub + 1) * 128], pT[bi][:], [("fpT", bi)], [("fxT", s2)])
            for c in range(22):
                res = []
                for gv in range(2):
                    cc = c + 22 * gv
                    p = cnt % 4
                    cnt += 1
                    for kc in range(8):
                        S.op("pe", lambda e, p=p, kc=kc, cc=cc, s2=s2: e.matmul(pU[p][:], lhsT=wup[:, kc, cc * 128:(cc + 1) * 128], rhs=xTs[s2][:, kc, :],
                                                                                start=(kc == 0), stop=(kc == 7)), reads=["wup", ("fxT", s2)], writes=[("fpU", p)])
                    ui = ucnt % 4
                    ucnt += 1
                    U, A = u[ui % 2], ca[ui]
                    eng = "dve" if gv == 0 else "pool"
                    S.op(eng, lambda e, U=U, cc=cc: e.tensor_copy(out=U[:, 0:2], in_=hal[:, cc, :]), reads=["hal"], writes=[("fu", ui % 2)])
                    copy_op(S, "act", U[:, 2:514], pU[p][:], [("fpU", p)], [("fu", ui % 2)])
                    S.op(eng, lambda e, U=U, cc=cc: e.tensor_copy(out=hal[:, cc, :], in_=U[:, 512:514]), reads=[("fu", ui % 2)], writes=["hal"])
                    S.op(eng, lambda e, U=U, A=A, cc=cc: e.tensor_scalar(out=A[:], in0=U[:, 0:512], scalar1=cw[:, cc, 0:1], scalar2=cb[:, cc:cc + 1],
                                                                       op0=ALU.mult, op1=ALU.add), reads=[("fu", ui % 2), "cw", "cb"], writes=[("fca", ui)])
                    S.op("dve", lambda e, U=U, A=A, cc=cc: e.scalar_tensor_tensor(out=A[:], in0=U[:, 1:513], scalar=cw[:, cc, 1:2], in1=A[:],
                                                                              op0=ALU.mult, op1=ALU.add), reads=[("fu", ui % 2), ("fca", ui), "cw"], writes=[("fca", ui)])
                    S.op("dve", lambda e, U=U, A=A, cc=cc: e.scalar_tensor_tensor(out=A[:], in0=U[:, 2:514], scalar=cw[:, cc, 2:3], in1=A[:],
                                                                              op0=ALU.mult, op1=ALU.add), reads=[("fu", ui % 2), ("fca", ui), "cw"], writes=[("fca", ui)])
                    res.append((A, ui))
                (Ag, ug), (Av, uv) = res
                S.op("act", lambda e, Ag=Ag: e.activation(out=Ag[:], in_=Ag[:], func=AF.Silu), reads=[("fca", ug)], writes=[("fca", ug)])
                S.op("dve", lambda e, Ag=Ag, Av=Av, c=c, s2=s2: e.tensor_tensor(out=hT[s2][:, c, :], in0=Ag[:], in1=Av[:], op=ALU.mult),
                     reads=[("fca", ug), ("fca", uv)], writes=[("fhT", s2)])
            for sub in range(4):
                t = st * 4 + sub
                xi = 0
                i = t % 2
                S.dma("sp", xt[xi][:], d["x1in"][t * 128:(t + 1) * 128, :], writes=[("fx", xi)])
                for half in range(2):
                    for c in range(22):
                        S.op("pe", lambda e, c=c, half=half, sub=sub, s2=s2: e.matmul(pD[half][:], lhsT=hT[s2][:, c, sub * 128:(sub + 1) * 128],
                                                                                      rhs=wdn[:, c, half * 512:(half + 1) * 512], start=(c == 0), stop=(c == 21)),
                             reads=[("fhT", s2), "wdn"], writes=[("fpD", half)])
                    S.op("dve", lambda e, i=i, half=half, xi=xi: e.scalar_tensor_tensor(out=y[i][:, half * 512:(half + 1) * 512], in0=xt[xi][:, half * 512:(half + 1) * 512],
                                                                                        scalar=ALPHA, in1=pD[half][:], op0=ALU.mult, op1=ALU.add),
                         reads=[("fx", xi), ("fpD", half)], writes=[("y", i)])
                ln_tile(S, C, stat, y[i][:], yo[i][:], lng, lnb, ("y", i), i)
                S.dma("sp", d["xout"][t * 128:(t + 1) * 128, :], yo[i][:], reads=[("y", i, "o"), ("y", i)], writes=[("xoutd", t)])
                if d.get("xout2") is not None:
                    S.dma("sp", d["xout2"][t * 128:(t + 1) * 128, :], yo[i][:], reads=[("y", i, "o"), ("y", i)], writes=[("xoutd2", t)])
        S.barrier()


def build_mixer():
    nc = bass.Bass("TRN2", target_bir_lowering=False)
    C = Ctx()
    C.nc = nc
    dt = {}

    def din(name, shape, dtp=F32):
        h = nc.dram_tensor(name, shape, dtp, kind="ExternalInput")
        dt[name] = h
        return h.ap()

    import os
    dump = os.environ.get("MK_DUMP", "").split(",")
    stop = int(os.environ.get("MK_STOP", "99"))

    def dscr(name, shape, dtp):
        if name in dump:
            h = nc.dram_tensor(name, shape, dtp, kind="ExternalOutput")
        else:
            h = nc.dram_tensor(name, shape, dtp)
        dt[name] = h
        return h.ap()

    d = {}
    d["xin"] = din("xin", [LOC, D])
    d["flag"] = din("flag", [128, 1])
    d["winp"] = din("winp", [8, 128, NCOLS])
    d["bgate"] = din("bgate", [1, NCOL_G])
    d["sinks"] = din("sinks", [1, 8])
    d["qg"] = din("qg", [1, 256])
    d["kvg"] = din("kvg", [1, 128])
    d["wuq"] = din("wuq", [256, 768])
    d["wukv"] = din("wukv", [128, 1024])
    d["wbr"] = din("wbr", [3, 512, 1024])
    d["wout"] = din("wout", [1024, 1024])
    d["ln1g"] = din("ln1g", [1, 1024])
    d["ln1b"] = din("ln1b", [1, 1024])
    d["tabext"] = din("tabext", [33, 32])
    d["ohext"] = din("ohext", [4, 33, GLEN])
    d["rope"] = din("rope", [LOC, 32])
    d["xout"] = nc.dram_tensor("xout", [OWN, D], F32, kind="ExternalOutput").ap()
    d["XT"] = dscr("XT", [8, 128, OWN], BF16)
    d["AQT"] = [dscr("AQT%d" % g, [4, 128, OWN], BF16) for g in range(3)]
    d["AKT"] = [dscr("AKT%d" % g, [4, 128, LOC], BF16) for g in range(3)]
    d["AV"] = [dscr("AV%d" % g, [LOC, 576], BF16) for g in range(3)]
    d["BQT"] = dscr("BQT", [4, 128, OWN], BF16)
    d["BKT"] = dscr("BKT", [2, 128, LOC], BF16)
    d["BV"] = dscr("BV", [LOC, 144], BF16)
    d["CRAW"] = dscr("CRAW", [LOC, 416], F32)
    d["CQT"] = dscr("CQT", [8, 96, OWN], BF16)
    d["CKT"] = dscr("CKT", [8, 96, LOC], BF16)
    d["CV"] = dscr("CV", [LOC, 576], BF16)
    d["OU"] = [dscr("OU%d" % i, [OWN, 576], F32) for i in range(5)]
    d["OT"] = [dscr("OT%d" % i, [4, 128, OWN], BF16) for i in range(3)]
    d["Gd"] = dscr("Gd", [32, 129 * (GLEN - 1)], F32)
    d["Gd_t"] = dt["Gd"]
    d["V_t"] = [dt["AV0"], dt["AV1"], dt["AV2"], dt["BV"]]
    d["OU_t"] = [dt["OU%d" % i] for i in range(5)]
    C.d = d
    with contextlib.ExitStack() as st:
        C.S = Sched(nc, st)
        C.sb = lambda n, s, t: st.enter_context(nc.sbuf_tensor(_u(n), s, t))
        setup_consts(C)
        phases = [lambda: phase_tables(C), lambda: phase_proj(C, True), lambda: phase_proj(C, False),
                  lambda: phase_banded(C, 0), lambda: phase_banded(C, 1), lambda: phase_banded(C, 2),
                  lambda: phase_combine(C, [0, 1, 2], 0, False), lambda: phase_banded(C, 3), lambda: phase_combine(C, [3], 1, True),
                  lambda: phase_mla_prep(C), lambda: phase_mla(C), lambda: phase_combine(C, [4], 2, False), lambda: phase_merge(C)]
        for pi_, p_ in enumerate(phases):
            if pi_ < stop:
                p_()
        C.S.barrier()
        print("mixer insts", C.S.n_inst, "waits", C.S.n_wait, {k: v for k, v in C.S.cnt.items()})
    return nc


def build_ffn():
    nc = bass.Bass("TRN2", target_bir_lowering=False)
    C = Ctx()
    C.nc = nc
    d = {}
    d["x1in"] = nc.dram_tensor("x1in", [OWN, D], F32, kind="ExternalInput").ap()
    d["xhalo"] = nc.dram_tensor("xhalo", [128, D], F32, kind="ExternalInput").ap()
    d["flag"] = nc.dram_tensor("flag", [128, 1], F32, kind="ExternalInput").ap()
    d["wup"] = nc.dram_tensor("wup", [1024, 2 * DFF], F32, kind="ExternalInput").ap()
    d["wdn"] = nc.dram_tensor("wdn", [DFF, 1024], F32, kind="ExternalInput").ap()
    d["convw"] = nc.dram_tensor("convw", [128, 44, 3], F32, kind="ExternalInput").ap()
    d["convb"] = nc.dram_tensor("convb", [128, 44], F32, kind="ExternalInput").ap()
    d["ln2g"] = nc.dram_tensor("ln2g", [1, 1024], F32, kind="ExternalInput").ap()
    d["ln2b"] = nc.dram_tensor("ln2b", [1, 1024], F32, kind="ExternalInput").ap()
    d["xout"] = nc.dram_tensor("xout", [OWN, D], F32, kind="ExternalOutput").ap()
    C.d = d
    with contextlib.ExitStack() as st:
        C.S = Sched(nc, st)
        C.sb = lambda n, s, t: st.enter_context(nc.sbuf_tensor(_u(n), s, t))
        setup_consts(C)
        phase_ffn(C)
        C.S.barrier()
    return nc


def build_fused():
    nc = bass.Bass("TRN2", target_bir_lowering=False)
    C = Ctx()
    C.nc = nc
    dt = {}

    def din(name, shape, dtp=F32):
        h = nc.dram_tensor(name, shape, dtp, kind="ExternalInput")
        dt[name] = h
        return h.ap()

    def dscr(name, shape, dtp):
        h = nc.dram_tensor(name, shape, dtp)
        dt[name] = h
        return h.ap()

    base = {}
    xinA = din("xinA", [LOC, D])
    xinB = din("xinB", [LOC, D])
    base["flag"] = din("flag", [128, 1])
    flag0 = din("flag0", [128, 1])
    base["tabext"] = din("tabext", [33, 32])
    base["ohext"] = din("ohext", [4, 33, GLEN])
    ropeA = din("ropeA", [LOC, 32])
    ropeB = din("ropeB", [LOC, 32])
    W = []
    for l in range(DEPTH):
        w = {}
        for nm, shp in (("winp", [8, 128, NCOLS]), ("bgate", [1, NCOL_G]), ("sinks", [1, 8]), ("qg", [1, 256]), ("kvg", [1, 128]),
                        ("wuq", [256, 768]), ("wukv", [128, 1024]), ("wbr", [3, 512, 1024]), ("wout", [1024, 1024]),
                        ("ln1g", [1, 1024]), ("ln1b", [1, 1024]), ("wup", [1024, 2 * DFF]), ("wdn", [DFF, 1024]),
                        ("convw", [128, 44, 3]), ("convb", [128, 44]), ("ln2g", [1, 1024]), ("ln2b", [1, 1024])):
            w[nm] = din("%s_%d" % (nm, l), shp)
        W.append(w)
    out = nc.dram_tensor("xout", [OWN, D], F32, kind="ExternalOutput").ap()
    scr = {}
    scr["XT"] = dscr("XT", [8, 128, OWN], BF16)
    scr["AQT"] = [dscr("AQT%d" % g, [4, 128, OWN], BF16) for g in range(3)]
    scr["AKT"] = [dscr("AKT%d" % g, [4, 128, LOC], BF16) for g in range(3)]
    scr["AV"] = [dscr("AV%d" % g, [LOC, 576], BF16) for g in range(3)]
    scr["BQT"] = dscr("BQT", [4, 128, OWN], BF16)
    scr["BKT"] = dscr("BKT", [2, 128, LOC], BF16)
    scr["BV"] = dscr("BV", [LOC, 144], BF16)
    scr["CRAW"] = dscr("CRAW", [LOC, 416], F32)
    scr["CQT"] = dscr("CQT", [8, 96, OWN], BF16)
    scr["CKT"] = dscr("CKT", [8, 96, LOC], BF16)
    scr["CV"] = dscr("CV", [LOC, 576], BF16)
    scr["OU"] = [dscr("OU%d" % i, [OWN, 576], F32) for i in range(5)]
    scr["OT"] = [dscr("OT%d" % i, [4, 128, OWN], BF16) for i in range(3)]
    scr["Gd"] = dscr("Gd", [32, 129 * (GLEN - 1)], F32)
    x1A = dscr("x1A", [OWN, D], F32)
    x1B = dscr("x1B", [OWN, D], F32)
    xin2 = dscr("xin2", [LOC, D], F32)
    xin2A = dscr("xin2A", [LOC, D], F32)
    x1C = dscr("x1C", [OWN, D], F32)
    x1D = dscr("x1D", [OWN, D], F32)

    def mk(l, **kw):
        dd = dict(base)
        dd.update(scr)
        dd.update(W[l])
        dd.update(kw)
        return dd

    with contextlib.ExitStack() as st:
        C.S = Sched(nc, st)
        S = C.S
        C.sb = lambda n, s, t: st.enter_context(nc.sbuf_tensor(_u(n), s, t))
        C.d = mk(0)
        setup_consts(C)
        flag1 = C.flag
        f0 = C.sb("flag0", [128, 1], F32)
        S.dma("sp", f0[:], flag0, writes=["flag0"])
        with contextlib.ExitStack() as zs:
            zt = zs.enter_context(nc.sbuf_tensor(_u("zt"), [128, 2048], F32))
            S.op("pool", lambda e: e.memset(zt[:], 0.0), writes=["zt"])
            for i in range(16):
                S.dma("sp", xin2A[i * 256:(i + 1) * 256, :].rearrange("(p a) c -> p (a c)", a=2), zt[:], reads=["zt"], writes=[("z", i)])
            S.barrier()
        phase_tables(C)

        def mixer(dd, flag, last_only=False):
            C.d = dd
            C.flag = flag
            C.last_only = last_only
            phase_proj(C, True)
            phase_proj(C, False)
            for typ in range(3):
                phase_banded(C, typ)
            phase_combine(C, [0, 1, 2], 0, False)
            phase_banded(C, 3)
            phase_combine(C, [3], 1, True)
            phase_mla_prep(C)
            phase_mla(C)
            phase_combine(C, [4], 2, False)
            phase_merge(C)
            C.last_only = False

        def ffn(dd, flag):
            C.d = dd
            C.flag = flag
            phase_ffn(C)

        mixer(mk(0, xin=xinA, rope=ropeA, xout=x1A), f0)
        ffn(mk(0, x1in=x1A, xhalo=x1A[OWN - 128:OWN, :], xout=xin2[0:OWN, :], xout2=xin2A[HIST:LOC, :]), f0)
        mixer(mk(0, xin=xinB, rope=ropeB, xout=x1B), flag1)
        ffn(mk(0, x1in=x1B, xhalo=x1A[OWN - 128:OWN, :], xout=xin2[HIST:LOC, :], xout2=None), flag1)
        mixer(mk(1, xin=xin2A, rope=ropeA, xout=x1C), f0, last_only=True)
        mixer(mk(1, xin=xin2, rope=ropeB, xout=x1D), flag1)
        ffn(mk(1, x1in=x1D, xhalo=x1C[OWN - 128:OWN, :], xout=out, xout2=None), flag1)
        S.barrier()
        print("fused insts", S.n_inst, "waits", S.n_wait, {k: v for k, v in S.cnt.items()})
    return nc

def _t5_bucket(n):
    n = np.maximum(n, 0)
    max_exact = 16
    scaled = np.log(np.maximum(n, 1).astype(np.float32) / np.float32(max_exact)) / np.float32(np.log(2048 / 16))
    large = max_exact + (scaled.astype(np.float32) * np.float32(16)).astype(np.int32)
    return np.where(n < max_exact, n, np.minimum(large, 31))


def _onehot_ext():
    oh = np.zeros((4, 33, GLEN), np.float32)
    for typ in range(4):
        dil = 1 if typ == 3 else A_DIL[typ]
        smax = 127 if typ == 3 else 128
        for i in range(GLEN):
            s = i - 127
            if 0 <= s <= smax:
                oh[typ, int(_t5_bucket(np.array(s * dil))), i] = 1.0
            else:
                oh[typ, 32, i] = 1.0
    return oh


def _w_in_perm():
    cols = []
    for g in range(3):
        cols += list(range(g * 1536, g * 1536 + 512))
        cols += list(range(g * 1536 + 512, g * 1536 + 1024))
    cols += list(range(4608, 5120))
    cols += list(range(5120, 5184)) * 2 + list(range(5184, 5248)) * 2
    for g in range(3):
        cols += list(range(g * 1536 + 1024, g * 1536 + 1536))
    cols += list(range(5248, 5376))
    cols += list(range(5376, 5792))
    cols += list(range(5792, 8864))
    assert len(cols) == NCOLS
    return np.array(cols)


_PROGS = {}


def _prog(name):
    if name not in _PROGS:
        _PROGS[name] = build_mixer() if name == "mixer" else build_ffn()
    return _PROGS[name]


def _rope_tab(own_start):
    pos = (np.arange(LOC) + own_start - HIST).astype(np.float32)
    inv = (np.float32(10000.0) ** (-np.arange(0, 32, 2, dtype=np.float32) / np.float32(32))).astype(np.float32)
    ang = pos[:, None] * inv[None, :]
    return np.concatenate([np.cos(ang), np.sin(ang)], axis=1).astype(np.float32)


def _core_inputs(c, x, rel_table, w_in, b_gate, sinks, q_norm_g, kv_norm_g, w_uq, w_ukv, w_branch, w_out,
                 ln1_g, ln1_b, w_ffn_up, conv_w, conv_b, w_ffn_down, ln2_g, ln2_b, shared):
    f = lambda a: np.ascontiguousarray(np.asarray(a, dtype=np.float32))
    b, h = c // 2, c % 2
    xinA = np.zeros((LOC, D), np.float32)
    xinB = np.zeros((LOC, D), np.float32)
    xinB[HIST:] = x[b, h * OWN:(h + 1) * OWN]
    if h == 1:
        xinA[HIST:] = x[b, 0:OWN]
        xinB[:HIST] = x[b, 0:OWN]
    m = {"xinA": xinA, "xinB": xinB, "flag": np.full((128, 1), float(h), np.float32), "flag0": np.zeros((128, 1), np.float32),
         "tabext": shared["tabext"], "ohext": shared["ohext"], "ropeA": shared["ropeA"], "ropeB": shared["ropeB"][h]}
    for l in range(DEPTH):
        m.update(shared["W"][l])
    return m


def kernel(x, rel_table, w_in, b_gate, sinks, q_norm_g, kv_norm_g, w_uq, w_ukv, w_branch, w_out,
           ln1_g, ln1_b, w_ffn_up, conv_w, conv_b, w_ffn_down, ln2_g, ln2_b):
    f = lambda a: np.ascontiguousarray(np.asarray(a, dtype=np.float32))
    x = f(x)
    perm = _w_in_perm()
    shared = {"tabext": np.concatenate([f(rel_table), np.full((1, 32), -30000.0, np.float32)], axis=0), "ohext": _onehot_ext(),
              "ropeA": _rope_tab(0), "ropeB": [_rope_tab(0), _rope_tab(OWN)], "W": []}
    for l in range(DEPTH):
        w = {"winp": f(f(w_in[l])[:, perm].reshape(8, 128, NCOLS)), "bgate": f(b_gate[l]).reshape(1, -1), "sinks": f(sinks[l]).reshape(1, -1),
             "qg": f(q_norm_g[l]).reshape(1, -1), "kvg": f(kv_norm_g[l]).reshape(1, -1), "wuq": f(w_uq[l]), "wukv": f(w_ukv[l]),
             "wbr": f(w_branch[l]), "wout": f(w_out[l]), "ln1g": f(ln1_g[l]).reshape(1, -1), "ln1b": f(ln1_b[l]).reshape(1, -1),
             "wup": f(w_ffn_up[l]), "wdn": f(w_ffn_down[l]),
             "convw": f(np.transpose(f(conv_w[l]).reshape(3, 44, 128), (2, 1, 0))), "convb": f(np.transpose(f(conv_b[l]).reshape(44, 128), (1, 0))),
             "ln2g": f(ln2_g[l]).reshape(1, -1), "ln2b": f(ln2_b[l]).reshape(1, -1)}
        shared["W"].append({"%s_%d" % (k, l): v for k, v in w.items()})
    cores = list(range(8))
    in_maps = [_core_inputs(c, x, rel_table, w_in, b_gate, sinks, q_norm_g, kv_norm_g, w_uq, w_ukv, w_branch, w_out,
                            ln1_g, ln1_b, w_ffn_up, conv_w, conv_b, w_ffn_down, ln2_g, ln2_b, shared) for c in cores]
    if "fused" not in _PROGS:
        _PROGS["fused"] = build_fused()
    res = run_bass_kernel_spmd(_PROGS["fused"], in_maps, core_ids=cores)
    out = np.stack([np.concatenate([res.results[2 * b]["xout"], res.results[2 * b + 1]["xout"]], axis=0) for b in range(NB)])
    return out.astype(np.float32)
```
